# Optimizing a Trainium2 kernel written in Bass

```python
import math
import jax
import jax.numpy as jnp
from jax import lax
import numpy as np

D_MODEL = 2048
BATCH = 1
SEQ = 8192
DEPTH = 2

CHUNK = 64
Q_BLOCK = 128
EPS = 1e-6
PLE_DIM = 256
D_FF = 5504
SSD_WIDTH = D_MODEL
SSD_HEAD_DIM = 64
SSD_HEADS = SSD_WIDTH // SSD_HEAD_DIM
SSD_GROUPS = 4
SSD_HPG = SSD_HEADS // SSD_GROUPS
SSD_STATE = 128
SSD_CONV = 4
SSD_CONV_DIM = SSD_WIDTH + 2 * SSD_GROUPS * SSD_STATE
HGRN_WIDTH = D_MODEL
HGRN_KDIM = 128
HGRN_HEADS = HGRN_WIDTH // HGRN_KDIM
HGRN_VDIM = HGRN_WIDTH // HGRN_HEADS
FOX_WIDTH = D_MODEL
FOX_HEAD_DIM = 128
FOX_HEADS = FOX_WIDTH // FOX_HEAD_DIM

N_EVEN = (DEPTH + 1) // 2
N_ODD = DEPTH // 2
AB_SPLITS = (
    SSD_WIDTH,
    SSD_WIDTH + SSD_CONV_DIM,
    SSD_WIDTH + SSD_CONV_DIM + SSD_HEADS,
    SSD_WIDTH + SSD_CONV_DIM + SSD_HEADS + HGRN_WIDTH,
    SSD_WIDTH + SSD_CONV_DIM + SSD_HEADS + 2 * HGRN_WIDTH,
    SSD_WIDTH + SSD_CONV_DIM + SSD_HEADS + 3 * HGRN_WIDTH,
)
AB_IN = SSD_WIDTH + SSD_CONV_DIM + SSD_HEADS + 4 * HGRN_WIDTH
AB_OUT = SSD_WIDTH + HGRN_WIDTH
FOX_IN = 3 * FOX_WIDTH + FOX_HEADS

kernel_name = "hybrid_ssd_hgrn2_fox_macaron_trunk"


def rmsnorm(x, w):
    xf = x.astype(jnp.float32)
    y = xf * lax.rsqrt(jnp.mean(xf * xf, axis=-1, keepdims=True) + EPS)
    return (y * w.astype(jnp.float32)).astype(x.dtype)


def swiglu_half(h, norm_w, w_in, w_out):
    gate, up = jnp.split(rmsnorm(h, norm_w) @ w_in, 2, axis=-1)
    return h + 0.5 * ((jax.nn.silu(gate) * up) @ w_out)


def causal_dwconv(x, w, b):
    k = w.shape[0]
    y = lax.conv_general_dilated(
        x, w[:, None, :].astype(x.dtype), window_strides=(1,), padding=[(k - 1, 0)],
        dimension_numbers=("NWC", "WIO", "NWC"), feature_group_count=x.shape[-1])
    return y + b.astype(x.dtype)


def chunk_state_scan(states, decay):
    def step(carry, inp):
        s_c, a_c = inp
        return carry * a_c + s_c, carry
    init = jnp.zeros_like(states[:, 0])
    _, prev = lax.scan(step, init, (jnp.moveaxis(states, 1, 0), jnp.moveaxis(decay, 1, 0)))
    return jnp.moveaxis(prev, 0, 1)


def ssd_scan(xs, b_in, c_in, dt, a_log, d_skip):
    bsz, t = xs.shape[:2]
    nc = t // CHUNK
    dtype = xs.dtype
    x = xs.reshape(bsz, nc, CHUNK, SSD_GROUPS, SSD_HPG, SSD_HEAD_DIM)
    bm = b_in.reshape(bsz, nc, CHUNK, SSD_GROUPS, SSD_STATE)
    cm = c_in.reshape(bsz, nc, CHUNK, SSD_GROUPS, SSD_STATE)
    dtc = dt.reshape(bsz, nc, CHUNK, SSD_GROUPS, SSD_HPG)
    a = -jnp.exp(a_log.astype(jnp.float32)).reshape(SSD_GROUPS, SSD_HPG)
    cum = jnp.cumsum(dtc * a, axis=2)
    xdt = (x * dtc[..., None]).astype(dtype)
    cum_t = jnp.moveaxis(cum, 2, -1)
    causal = jnp.tril(jnp.ones((CHUNK, CHUNK), dtype=bool))
    decay = jnp.exp(jnp.where(causal, cum_t[..., :, None] - cum_t[..., None, :], -jnp.inf))
    cb = jnp.einsum("bclgn,bcsgn->bcgls", cm, bm).astype(jnp.float32)
    w = (cb[:, :, :, None] * decay).astype(dtype)
    y_diag = jnp.einsum("bcghls,bcsghp->bclghp", w, xdt)
    to_end = jnp.exp(cum[:, :, -1:] - cum)
    states = jnp.einsum("bclgn,bclghp->bcghpn", bm, (xdt * to_end[..., None]).astype(dtype))
    chunk_decay = jnp.exp(cum[:, :, -1]).astype(dtype)[..., None, None]
    prev = chunk_state_scan(states, chunk_decay)
    y_off = jnp.einsum("bclgn,bcghpn->bclghp", cm, prev) * jnp.exp(cum)[..., None].astype(dtype)
    y = y_diag + y_off + x * d_skip.reshape(SSD_GROUPS, SSD_HPG, 1).astype(dtype)
    return y.reshape(bsz, t, SSD_WIDTH)


def hgrn2_scan(q, f_raw, v, lb):
    bsz, t = q.shape[:2]
    nc = t // CHUNK
    dtype = q.dtype
    shp = (bsz, nc, CHUNK, HGRN_HEADS, HGRN_KDIM)
    lbh = lb.astype(jnp.float32).reshape(HGRN_HEADS, HGRN_KDIM)
    f = lbh + (1.0 - lbh) * jax.nn.sigmoid(f_raw.astype(jnp.float32))
    k = (1.0 - f).reshape(shp)
    cum = jnp.cumsum(jnp.log(f).reshape(shp), axis=2)
    qf = jax.nn.silu(q.astype(jnp.float32)).reshape(shp)
    vc = v.reshape(bsz, nc, CHUNK, HGRN_HEADS, HGRN_VDIM)
    mid = cum[:, :, CHUNK // 2 - 1:CHUNK // 2]
    q_rel = (qf * jnp.exp(cum - mid)).astype(dtype)
    k_rel = (k * jnp.exp(mid - cum)).astype(dtype)
    causal = jnp.tril(jnp.ones((CHUNK, CHUNK), dtype=bool))
    att = jnp.where(causal, jnp.einsum("bclhk,bcshk->bchls", q_rel, k_rel), 0)
    o_intra = jnp.einsum("bchls,bcshv->bclhv", att, vc)
    k_end = (k * jnp.exp(cum[:, :, -1:] - cum)).astype(dtype)
    states = jnp.einsum("bclhk,bclhv->bchkv", k_end, vc)
    chunk_decay = jnp.exp(cum[:, :, -1]).astype(dtype)[..., None]
    prev = chunk_state_scan(states, chunk_decay)
    o_inter = jnp.einsum("bclhk,bchkv->bclhv", (qf * jnp.exp(cum)).astype(dtype), prev)
    return (o_intra + o_inter).reshape(bsz, t, HGRN_HEADS, HGRN_VDIM)


def ssd_hgrn_mixer(hn, w_in, conv_w, conv_b, dt_bias, a_log, d_skip, ssd_norm_w, lb, hgrn_norm_w, w_out):
    bsz, t, _ = hn.shape
    z, xbc, dt_raw, q, f_raw, v, g = jnp.split(hn @ w_in, AB_SPLITS, axis=-1)
    xbc = jax.nn.silu(causal_dwconv(xbc, conv_w, conv_b))
    xs, b_in, c_in = jnp.split(xbc, [SSD_WIDTH, SSD_WIDTH + SSD_GROUPS * SSD_STATE], axis=-1)
    dt = jax.nn.softplus(dt_raw.astype(jnp.float32) + dt_bias.astype(jnp.float32))
    y_a = ssd_scan(xs, b_in, c_in, dt, a_log, d_skip) * jax.nn.silu(z)
    grp = SSD_WIDTH // SSD_GROUPS
    y_a = rmsnorm(y_a.reshape(bsz, t, SSD_GROUPS, grp), ssd_norm_w.reshape(SSD_GROUPS, grp)).reshape(bsz, t, SSD_WIDTH)
    y_b = hgrn2_scan(q.reshape(bsz, t, HGRN_HEADS, HGRN_KDIM), f_raw.reshape(bsz, t, HGRN_HEADS, HGRN_KDIM),
                     v.reshape(bsz, t, HGRN_HEADS, HGRN_VDIM), lb)
    y_b = rmsnorm(y_b, hgrn_norm_w.reshape(HGRN_HEADS, HGRN_VDIM)).reshape(bsz, t, HGRN_WIDTH) * jax.nn.silu(g)
    return jnp.concatenate([y_a, y_b], axis=-1) @ w_out


def fox_attention(hn, w_in, b_f, w_out):
    bsz, t, _ = hn.shape
    q, k, v, f_raw = jnp.split(hn @ w_in, [FOX_WIDTH, 2 * FOX_WIDTH, 3 * FOX_WIDTH], axis=-1)
    q = q.reshape(bsz, t, FOX_HEADS, FOX_HEAD_DIM)
    k = k.reshape(bsz, t, FOX_HEADS, FOX_HEAD_DIM)
    v = v.reshape(bsz, t, FOX_HEADS, FOX_HEAD_DIM)
    log_f = jax.nn.log_sigmoid(f_raw.astype(jnp.float32) + b_f.astype(jnp.float32))
    dcum = jnp.swapaxes(jnp.cumsum(log_f, axis=1), 1, 2)
    scale = FOX_HEAD_DIM ** -0.5
    q_idx = jnp.arange(Q_BLOCK)
    outs = []
    for blk in range(t // Q_BLOCK):
        s0 = blk * Q_BLOCK
        s1 = s0 + Q_BLOCK
        logits = jnp.einsum("bqhd,bkhd->bhqk", q[:, s0:s1], k[:, :s1]).astype(jnp.float32) * scale
        logits = logits + dcum[:, :, s0:s1, None] - dcum[:, :, None, :s1]
        mask = (s0 + q_idx)[:, None] >= jnp.arange(s1)[None, :]
        probs = jax.nn.softmax(jnp.where(mask, logits, -jnp.inf), axis=-1).astype(v.dtype)
        outs.append(jnp.einsum("bhqk,bkhd->bqhd", probs, v[:, :s1]))
    o = jnp.concatenate(outs, axis=1).reshape(bsz, t, FOX_WIDTH)
    return o @ w_out


def ple_add(h, p_i, gate_norm_w, w_gate, w_up, post_norm_w):
    emb = rmsnorm(p_i @ w_up, post_norm_w)
    gate = jax.nn.sigmoid(rmsnorm(h, gate_norm_w) @ w_gate)
    return h + emb * gate


def setup_inputs(seed: int = 0) -> dict:
    key = jax.random.key(seed)
    ks = iter(jax.random.split(key, 40))
    f32 = jnp.float32
    D = D_MODEL

    def nrm(shape, scale):
        return jax.random.normal(next(ks), shape, f32) * scale

    def gain(shape):
        return 1.0 + 0.05 * jax.random.normal(next(ks), shape, f32)

    x = nrm((BATCH, SEQ, D), 1.0)
    p = nrm((DEPTH, BATCH, SEQ, PLE_DIM), 1.0)
    ffn1_norm = gain((DEPTH, D))
    ffn1_w_in = nrm((DEPTH, D, 2 * D_FF), D ** -0.5)
    ffn1_w_out = nrm((DEPTH, D_FF, D), D_FF ** -0.5)
    mix_norm = gain((DEPTH, D))
    ab_w_in = nrm((N_EVEN, D, AB_IN), D ** -0.5)
    ssd_conv_w = nrm((N_EVEN, SSD_CONV, SSD_CONV_DIM), SSD_CONV ** -0.5)
    ssd_conv_b = nrm((N_EVEN, SSD_CONV_DIM), 0.02)
    dt0 = jnp.exp(jax.random.uniform(next(ks), (N_EVEN, SSD_HEADS), f32, math.log(1e-3), math.log(1e-1)))
    ssd_dt_bias = dt0 + jnp.log(-jnp.expm1(-dt0))
    ssd_a_log = jnp.log(jax.random.uniform(next(ks), (N_EVEN, SSD_HEADS), f32, 1.0, 16.0))
    ssd_d = gain((N_EVEN, SSD_HEADS))
    ssd_norm = gain((N_EVEN, SSD_WIDTH))
    hgrn_lb_logits = nrm((DEPTH + 1, HGRN_WIDTH), 0.1)
    hgrn_norm = gain((N_EVEN, HGRN_WIDTH))
    ab_w_out = nrm((N_EVEN, AB_OUT, D), AB_OUT ** -0.5)
    fox_w_in = nrm((N_ODD, D, FOX_IN), D ** -0.5)
    fox_b_f = 2.0 + nrm((N_ODD, FOX_HEADS), 0.5)
    fox_w_out = nrm((N_ODD, FOX_WIDTH, D), FOX_WIDTH ** -0.5)
    ffn2_norm = gain((DEPTH, D))
    ffn2_w_in = nrm((DEPTH, D, 2 * D_FF), D ** -0.5)
    ffn2_w_out = nrm((DEPTH, D_FF, D), D_FF ** -0.5)
    ple_gate_norm = gain((DEPTH, D))
    ple_w_gate = nrm((DEPTH, D, D), D ** -0.5)
    ple_w_up = nrm((DEPTH, PLE_DIM, D), PLE_DIM ** -0.5)
    ple_norm = gain((DEPTH, D))
    final_norm = gain((D,))
    return {
        "x": x, "p": p,
        "ffn1_norm": ffn1_norm, "ffn1_w_in": ffn1_w_in, "ffn1_w_out": ffn1_w_out,
        "mix_norm": mix_norm, "ab_w_in": ab_w_in, "ssd_conv_w": ssd_conv_w, "ssd_conv_b": ssd_conv_b,
        "ssd_dt_bias": ssd_dt_bias, "ssd_a_log": ssd_a_log, "ssd_d": ssd_d, "ssd_norm": ssd_norm,
        "hgrn_lb_logits": hgrn_lb_logits, "hgrn_norm": hgrn_norm, "ab_w_out": ab_w_out,
        "fox_w_in": fox_w_in, "fox_b_f": fox_b_f, "fox_w_out": fox_w_out,
        "ffn2_norm": ffn2_norm, "ffn2_w_in": ffn2_w_in, "ffn2_w_out": ffn2_w_out,
        "ple_gate_norm": ple_gate_norm, "ple_w_gate": ple_w_gate, "ple_w_up": ple_w_up, "ple_norm": ple_norm,
        "final_norm": final_norm,
    }


def reference(x, p, ffn1_norm, ffn1_w_in, ffn1_w_out, mix_norm, ab_w_in, ssd_conv_w, ssd_conv_b,
              ssd_dt_bias, ssd_a_log, ssd_d, ssd_norm, hgrn_lb_logits, hgrn_norm, ab_w_out,
              fox_w_in, fox_b_f, fox_w_out, ffn2_norm, ffn2_w_in, ffn2_w_out,
              ple_gate_norm, ple_w_gate, ple_w_up, ple_norm, final_norm):
    lb_all = jnp.cumsum(jax.nn.softmax(hgrn_lb_logits.astype(jnp.float32), axis=0), axis=0)
    h = x
    for i in range(DEPTH):
        h = swiglu_half(h, ffn1_norm[i], ffn1_w_in[i], ffn1_w_out[i])
        hn = rmsnorm(h, mix_norm[i])
        j = i // 2
        if i % 2 == 0:
            h = h + ssd_hgrn_mixer(hn, ab_w_in[j], ssd_conv_w[j], ssd_conv_b[j], ssd_dt_bias[j], ssd_a_log[j],
                                   ssd_d[j], ssd_norm[j], lb_all[i], hgrn_norm[j], ab_w_out[j])
        else:
            h = h + fox_attention(hn, fox_w_in[j], fox_b_f[j], fox_w_out[j])
        h = swiglu_half(h, ffn2_norm[i], ffn2_w_in[i], ffn2_w_out[i])
        h = ple_add(h, p[i], ple_gate_norm[i], ple_w_gate[i], ple_w_up[i], ple_norm[i])
    return rmsnorm(h, final_norm)
```

```python
import numpy as np
from contextlib import ExitStack
import concourse.bass as bass
import concourse.mybir as mybir
from concourse.bass_utils import run_bass_kernel_spmd

F32 = mybir.dt.float32
BF16 = mybir.dt.bfloat16
AF = mybir.ActivationFunctionType
ALU = mybir.AluOpType
AX = mybir.AxisListType

NCORES = 8
D = 2048
SEQ = 8192
TOK = SEQ // NCORES
TT = TOK // 128
KD = D // 128
DFF = 5504
NF = DFF // 128
EPS = 1e-6
SAME_ENGINE_SYNC = True


class Prog:
    ENG = ("pe", "act", "dve", "pool", "sp")
    NSLOT = 8

    def __init__(self, nc, es):
        self.nc = nc
        self.ops = {e: [] for e in self.ENG}
        self.sem = {}
        self.cnt = {}
        for e in ("pe", "act", "dve", "pool"):
            self.sem[e] = es.enter_context(nc.semaphore("s_" + e))
            self.cnt[e] = 0
        self.slot_n = {}
        for q in ("sp", "pool", "act"):
            for i in range(self.NSLOT):
                k = "d_%s%d" % (q, i)
                self.sem[k] = es.enter_context(nc.semaphore(k))
                self.cnt[k] = 0
            self.slot_n[q] = 0
        self.known = {e: {} for e in self.ENG}
        self.log = []
        self.lastw = {}
        self.readers = {}

    def _wait(self, eng, toks):
        best = {}
        for (sk, val) in toks:
            if sk == eng and not SAME_ENGINE_SYNC:
                continue
            if best.get(sk, 0) < val:
                best[sk] = val
        for sk, val in best.items():
            if self.known[eng].get(sk, 0) >= val:
                continue
            self.known[eng][sk] = val
            sem = self.sem[sk]
            self.log.append((eng, "wait", sk, val))
            self.ops[eng].append(lambda e, sem=sem, val=val: e.wait_ge(sem, val))

    def _deps(self, reads, writes):
        deps = []
        for k in reads:
            if k in self.lastw:
                deps.append(self.lastw[k])
        for k in writes:
            if k in self.lastw:
                deps.append(self.lastw[k])
            deps.extend(self.readers.get(k, {}).items())
        return deps

    def _record(self, tok, reads, writes):
        for k in reads:
            r = self.readers.setdefault(k, {})
            if r.get(tok[0], 0) < tok[1]:
                r[tok[0]] = tok[1]
        for k in writes:
            self.lastw[k] = tok
            self.readers[k] = {}

    def op(self, eng, fn, reads=(), writes=()):
        bk = [k_ for k_ in reads if k_.startswith("bank")]
        if bk:
            reads = [k_ for k_ in reads if not k_.startswith("bank")]
            writes = list(writes) + [k_ for k_ in bk if k_ not in writes]
        self._wait(eng, self._deps(reads, writes))
        self.cnt[eng] += 1
        tok = (eng, self.cnt[eng])
        sem = self.sem[eng]
        self.ops[eng].append(lambda e, fn=fn, sem=sem: fn(e).then_inc(sem, 1))
        self.log.append((eng, "op", tok, tuple(reads), tuple(writes)))
        self._record(tok, reads, writes)
        return tok

    def dma(self, q, out, in_, reads=(), writes=()):
        i = self.slot_n[q] % self.NSLOT
        self.slot_n[q] += 1
        sk = "d_%s%d" % (q, i)
        deps = self._deps(reads, writes)
        if self.cnt[sk] > 0:
            deps.append((sk, self.cnt[sk]))
        self._wait(q, deps)
        self.cnt[sk] += 16
        tok = (sk, self.cnt[sk])
        sem = self.sem[sk]
        self.ops[q].append(lambda e, out=out, in_=in_, sem=sem: e.dma_start(out=out, in_=in_).then_inc(sem, 16))
        self.log.append((q, "dma", tok, tuple(reads), tuple(writes)))
        self._record(tok, reads, writes)
        return tok

    def wait_all(self, eng, keys):
        toks = [self.lastw[k] for k in keys if k in self.lastw]
        self._wait(eng, toks)

    def emit(self):
        nc = self.nc
        with nc.Block() as block:
            @block.tensor
            def _(e):
                for f in self.ops["pe"]:
                    f(e)

            @block.scalar
            def _(e):
                for f in self.ops["act"]:
                    f(e)

            @block.vector
            def _(e):
                for f in self.ops["dve"]:
                    f(e)

            @block.gpsimd
            def _(e):
                for f in self.ops["pool"]:
                    f(e)

            @block.sync
            def _(e):
                for f in self.ops["sp"]:
                    f(e)


class K:
    def __init__(self, nc, es, cfg):
        self.nc = nc
        self.es = es
        self.cfg = cfg
        self.P = Prog(nc, es)
        self.uid = 0
        self.dram = {}
        self.ARENA = 204 * 1024
        self.arena = es.enter_context(nc.sbuf_tensor("arena", [128, self.ARENA // 4], F32))
        self.psum = [es.enter_context(nc.psum_tensor("ps%d" % i, [128, 512], F32)) for i in range(8)]

    def din(self, name, shape, dt=F32):
        t = self.nc.dram_tensor(name, list(shape), dt, kind="ExternalInput").ap()
        self.dram[name] = t
        return t

    def dout(self, name, shape, dt=F32):
        t = self.nc.dram_tensor(name, list(shape), dt, kind="ExternalOutput").ap()
        self.dram[name] = t
        return t

    def dint(self, name, shape, dt):
        t = self.nc.dram_tensor(name, list(shape), dt, kind="Internal").ap()
        self.dram[name] = t
        return t

    def view(self, off, shape, dt, parts=128):
        esz = 2 if dt == BF16 else 4
        n = 1
        for s in shape[1:]:
            n *= s
        assert off % 4 == 0 and (n * esz) % 4 == 0
        assert off + n * esz <= self.ARENA, (off, n, esz)
        a = self.arena[0:shape[0], off // 4: off // 4 + (n * esz) // 4]
        if dt != F32:
            a = a.bitcast(dt)
        if len(shape) == 3:
            a = a.rearrange("p (a b) -> p a b", a=shape[1], b=shape[2])
        elif len(shape) == 4:
            a = a.rearrange("p (a b c) -> p a b c", a=shape[1], b=shape[2], c=shape[3])
        return a

    def psv(self, i, dt=F32):
        a = self.psum[i][:, :]
        if dt != F32:
            a = a.bitcast(dt)
        return a


OFF_H = 0
OFF_HNT = 64 * 1024
OFF_CONST = 96 * 1024
OFF_HNTM = 106 * 1024
OFF_WORK = 114 * 1024
FFN_G = 4


def _groups(n, g):
    out = []
    i = 0
    while i < n:
        out.append(list(range(i, min(n, i + g))))
        i += g
    return out


class KT(K):
    def setup_common(self):
        P = self.P
        self.h = self.view(OFF_H, [128, TT, D], F32)
        self.hnT = self.view(OFF_HNT, [128, KD, TOK], BF16)
        self.ident = self.view(OFF_CONST, [128, 128], BF16)
        self.cw = self.view(OFF_CONST + 256, [128, 128], F32)
        self.stat = self.view(OFF_CONST + 768, [128, 64], F32)
        self.stat2 = self.view(OFF_CONST + 1024, [128, 64], F32)
        self.hn_tm = [self.view(OFF_HNTM + i * 4096, [128, D], BF16) for i in range(2)]
        d_ident = self.din("ident", [128, 128])
        d_cw = self.din("cw", [128, 128])
        P.dma("pool", self.ident, d_ident, writes=["ident"])
        P.dma("sp", self.cw, d_cw, writes=["cw"])

    def fence(self):
        P = self.P
        toks = [(e, P.cnt[e]) for e in ("pe", "act", "dve", "pool") if P.cnt[e] > 0]
        toks += [(k, v) for k, v in P.cnt.items() if (k.startswith("d_") or k == "cc") and v > 0]
        for e in P.ENG:
            P._wait(e, toks)
        P.lastw = {}
        P.readers = {}

    def load_x(self):
        x = self.din("x", [TOK, D])
        for t in range(TT):
            self.P.dma("sp", self.h[:, t, :], x[t * 128:(t + 1) * 128, :], writes=["h%d" % t])

    def rstd_from_ss(self, ss, rs, n):
        P = self.P
        P.op("dve", lambda e: e.tensor_scalar(out=rs, in0=ss, scalar1=1.0 / D, scalar2=EPS, op0=ALU.mult, op1=ALU.add),
             reads=["ss"], writes=["rs"])
        P.op("act", lambda e: e.activation(out=rs, in_=rs, func=AF.Sqrt), reads=["rs"], writes=["rs"])
        P.op("dve", lambda e: e.reciprocal(out=rs, in_=rs), reads=["rs"], writes=["rs"])

    def norm_T(self, widx):
        P = self.P
        w = self.cw[:, widx * 16:(widx + 1) * 16]
        ss = self.stat[:, 0:TT]
        rs = self.stat2[:, 0:TT]
        for t in range(TT):
            hb = self.hn_tm[t % 2]
            P.op("act", lambda e, t=t, hb=hb: e.activation(out=hb, in_=self.h[:, t, :], func=AF.Square,
                                                         accum_out=self.stat[:, t:t + 1]),
                 reads=["h%d" % t], writes=["hb%d" % (t % 2), "ss"])
        self.rstd_from_ss(ss, rs, TT)
        for t in range(TT):
            hb = self.hn_tm[t % 2]
            P.op("act", lambda e, t=t, hb=hb: e.activation(out=hb, in_=self.h[:, t, :], func=AF.Copy,
                                                         scale=self.stat2[:, t:t + 1]),
                 reads=["h%d" % t, "rs"], writes=["hb%d" % (t % 2)])
            for half in range(2):
                bank = 6 + half
                psb = self.psv(bank, BF16)

                def tr(e, hb=hb, half=half, psb=psb):
                    ins = None
                    for j in range(8):
                        kk = half * 8 + j
                        ins = e.transpose(psb[:, j * 128:(j + 1) * 128], hb[:, kk * 128:(kk + 1) * 128], self.ident)
                    return ins
                P.op("pe", tr, reads=["hb%d" % (t % 2), "ident"], writes=["bank%d" % bank])
                P.op("dve", lambda e, t=t, half=half, psb=psb: e.tensor_tensor(
                    out=self.hnT[:, half * 8:(half + 1) * 8, t * 128:(t + 1) * 128],
                    in0=psb.rearrange("p (a b) -> p a b", a=8, b=128),
                    in1=w[:, half * 8:(half + 1) * 8].unsqueeze(2).broadcast_to([128, 8, 128]),
                    op=ALU.mult), reads=["bank%d" % bank, "cw"], writes=["hnT"])

    def ffn(self, w1, w2, widx):
        P = self.P
        self.fence()
        self.norm_T(widx)
        W = OFF_WORK
        w1b = [self.view(W + i * 8192, [128, 2, KD, 128], BF16) for i in range(3)]
        aT = [self.view(W + 24576 + i * 8192, [128, FFN_G, TOK], BF16) for i in range(2)]
        w2b = [self.view(W + 40960 + i * 16384, [128, FFN_G, D], BF16) for i in range(2)]
        tmp = [self.view(W + 73728 + i * 2048, [128, 512], F32) for i in range(2)]
        groups = _groups(NF, FFN_G)
        cnt = [0, 0]

        def load_w1(j):
            if j < NF:
                P.dma("pool", w1b[j % 3].rearrange("p a k c -> p (a k c)"), w1[j], writes=["w1b%d" % (j % 3)])

        def load_w2(g):
            if g < len(groups):
                for jj, j in enumerate(groups[g]):
                    P.dma("pool", w2b[g % 2][:, jj, :], w2[j], writes=["w2b%d_%d" % (g % 2, jj)])

        def s1(g):
            for jj, j in enumerate(groups[g]):
                wb = w1b[j % 3]
                for half in range(2):
                    c = cnt[0] % 2
                    cnt[0] += 1
                    bg, bu = 2 * c, 2 * c + 1
                    tsl = slice(half * 512, (half + 1) * 512)

                    def mm(e, wb=wb, bg=bg, bu=bu, tsl=tsl):
                        ins = None
                        for a, b in ((0, bg), (1, bu)):
                            for k in range(KD):
                                ins = e.matmul(self.psv(b), wb[:, a, k, :], self.hnT[:, k, tsl], start=(k == 0), stop=(k == KD - 1))
                        return ins
                    P.op("pe", mm, reads=["w1b%d" % (j % 3), "hnT"], writes=["bank%d" % bg, "bank%d" % bu])
                    P.op("act", lambda e, c=c, bg=bg: e.activation(out=tmp[c], in_=self.psv(bg), func=AF.Silu),
                         reads=["bank%d" % bg], writes=["tmp%d" % c])
                    P.op("dve", lambda e, c=c, bu=bu, g=g, jj=jj, tsl=tsl: e.tensor_tensor(
                        out=aT[g % 2][:, jj, tsl], in0=tmp[c], in1=self.psv(bu), op=ALU.mult),
                        reads=["tmp%d" % c, "bank%d" % bu], writes=["aT%d_%d_%d" % (g % 2, jj, half)])
                    if half == 1:
                        load_w1(j + 3)

        def s2(g):
            n = len(groups[g])
            rk = ["aT%d_%d_%d" % (g % 2, jj, hh) for jj in range(n) for hh in range(2)] + ["w2b%d_%d" % (g % 2, jj) for jj in range(n)]
            for db in range(4):
                for t in range(TT):
                    b = 4 + cnt[1] % 2
                    cnt[1] += 1

                    def mm(e, g=g, n=n, b=b, db=db, t=t):
                        ins = None
                        for jj in range(n):
                            ins = e.matmul(self.psv(b), aT[g % 2][:, jj, t * 128:(t + 1) * 128],
                                           w2b[g % 2][:, jj, db * 512:(db + 1) * 512], start=(jj == 0), stop=(jj == n - 1))
                        return ins
                    P.op("pe", mm, reads=rk, writes=["bank%d" % b])
                    hs = self.h[:, t, db * 512:(db + 1) * 512]
                    P.op("dve", lambda e, b=b, hs=hs: e.scalar_tensor_tensor(out=hs, in0=self.psv(b), scalar=0.5, in1=hs,
                                                                             op0=ALU.mult, op1=ALU.add),
                         reads=["bank%d" % b, "h%d" % t], writes=["h%d" % t])

        ng = len(groups)
        for j in range(3):
            load_w1(j)
        load_w2(0)
        load_w2(1)
        for g in range(ng):
            s1(g)
            if g >= 1:
                s2(g - 1)
                load_w2(g + 1)
        s2(ng - 1)

    def ple(self, wg, wup, pT, wpost, widx):
        P = self.P
        self.fence()
        self.norm_T(widx)
        W = OFF_WORK
        wgb = [self.view(W + i * 16384, [128, KD, 512], BF16) for i in range(2)]
        wupb = self.view(W + 32768, [128, 2, D], BF16)
        pTb = self.view(W + 40960, [128, 2, TOK], BF16)
        wpb = self.view(W + 45056, [128, D], F32)
        tS = [self.view(W + 53248 + i * 2048, [128, 512], F32) for i in range(2)]
        tE = [self.view(W + 57344 + i * 2048, [128, 512], F32) for i in range(2)]
        P.dma("pool", wupb, wup.rearrange("(k p) d -> p k d", p=128), writes=["wupb"])
        P.dma("pool", pTb, pT.rearrange("(k p) t -> p k t", p=128), writes=["pTb"])
        P.dma("sp", wpb, wpost, writes=["wpb"])
        P.dma("pool", wgb[0].rearrange("p k c -> p (k c)"), wg[0], writes=["wgb0"])
        ssq = self.stat[:, 0:32]
        cnt = 0
        for t in range(TT):
            for db in range(4):
                b = cnt % 2
                cnt += 1

                def mm(e, b=b, t=t, db=db):
                    ins = None
                    for k in range(2):
                        ins = e.matmul(self.psv(b), pTb[:, k, t * 128:(t + 1) * 128], wupb[:, k, db * 512:(db + 1) * 512],
                                       start=(k == 0), stop=(k == 1))
                    return ins
                P.op("pe", mm, reads=["wupb", "pTb"], writes=["bank%d" % b])
                P.op("act", lambda e, b=b, t=t, db=db: e.activation(out=tS[b], in_=self.psv(b), func=AF.Square,
                                                                   accum_out=self.stat[:, t * 4 + db:t * 4 + db + 1]),
                     reads=["bank%d" % b], writes=["tS%d" % b, "ssq"])
        P.op("dve", lambda e: e.tensor_reduce(out=self.stat2[:, 32:40], in_=ssq.rearrange("p (t d) -> p t d", t=8, d=4),
                                              axis=AX.X, op=ALU.add), reads=["ssq"], writes=["ss"])
        rs = self.stat2[:, 40:48]
        self.rstd_from_ss(self.stat2[:, 32:40], rs, TT)
        cnt = 0
        for db in range(4):
            if db + 1 < 4:
                P.dma("pool", wgb[(db + 1) % 2].rearrange("p k c -> p (k c)"), wg[db + 1], writes=["wgb%d" % ((db + 1) % 2)])
            for t in range(TT):
                c = cnt % 2
                cnt += 1
                bA, bB = 2 * c, 2 * c + 1
                dsl = slice(db * 512, (db + 1) * 512)

                def mm(e, bA=bA, bB=bB, t=t, db=db, dsl=dsl):
                    ins = None
                    for k in range(KD):
                        ins = e.matmul(self.psv(bA), self.hnT[:, k, t * 128:(t + 1) * 128], wgb[db % 2][:, k, :],
                                       start=(k == 0), stop=(k == KD - 1))
                    for k in range(2):
                        ins = e.matmul(self.psv(bB), pTb[:, k, t * 128:(t + 1) * 128], wupb[:, k, dsl], start=(k == 0), stop=(k == 1))
                    return ins
                P.op("pe", mm, reads=["hnT", "wgb%d" % (db % 2), "wupb", "pTb"], writes=["bank%d" % bA, "bank%d" % bB])
                P.op("act", lambda e, c=c, bA=bA: e.activation(out=tS[c], in_=self.psv(bA), func=AF.Sigmoid),
                     reads=["bank%d" % bA], writes=["tS%d" % c])
                P.op("dve", lambda e, c=c, bB=bB, t=t, dsl=dsl: e.scalar_tensor_tensor(
                    out=tE[c], in0=self.psv(bB), scalar=self.stat2[:, 40 + t:41 + t], in1=wpb[:, dsl], op0=ALU.mult, op1=ALU.mult),
                    reads=["bank%d" % bB, "rs", "wpb"], writes=["tE%d" % c])
                P.op("dve", lambda e, c=c: e.tensor_tensor(out=tE[c], in0=tE[c], in1=tS[c], op=ALU.mult),
                     reads=["tE%d" % c, "tS%d" % c], writes=["tE%d" % c])
                hs = self.h[:, t, dsl]
                P.op("pool", lambda e, c=c, hs=hs: e.tensor_tensor(out=hs, in0=hs, in1=tE[c], op=ALU.add),
                     reads=["tE%d" % c, "h%d" % t], writes=["h%d" % t])

    def final(self, wfin, out):
        P = self.P
        self.fence()
        W = OFF_WORK
        wfb = self.view(W, [128, D], F32)
        ob = [self.view(W + 8192 + i * 8192, [128, D], F32) for i in range(2)]
        P.dma("sp", wfb, wfin, writes=["wfb"])
        for t in range(TT):
            P.op("act", lambda e, t=t: e.activation(out=ob[t % 2], in_=self.h[:, t, :], func=AF.Square,
                                                   accum_out=self.stat[:, t:t + 1]),
                 reads=["h%d" % t], writes=["ob%d" % (t % 2), "ss"])
        self.rstd_from_ss(self.stat[:, 0:TT], self.stat2[:, 0:TT], TT)
        for t in range(TT):
            P.op("dve", lambda e, t=t: e.scalar_tensor_tensor(out=ob[t % 2], in0=self.h[:, t, :], scalar=self.stat2[:, t:t + 1],
                                                            in1=wfb, op0=ALU.mult, op1=ALU.mult),
                 reads=["h%d" % t, "rs", "wfb"], writes=["ob%d" % (t % 2)])
            P.dma("sp", out[t * 128:(t + 1) * 128, :], ob[t % 2], reads=["ob%d" % (t % 2)], writes=["out%d" % t])
        P.wait_all("sp", ["out%d" % t for t in range(TT)])

    def dump_h(self, out):
        P = self.P
        for t in range(TT):
            P.dma("sp", out[t * 128:(t + 1) * 128, :], self.h[:, t, :], reads=["h%d" % t], writes=["out%d" % t])
        P.wait_all("sp", ["out%d" % t for t in range(TT)])


def prep_w1(w_in):
    g = w_in[:, :DFF].reshape(KD, 128, NF, 128)
    u = w_in[:, DFF:].reshape(KD, 128, NF, 128)
    s = np.stack([g, u], axis=0)
    return np.ascontiguousarray(s.transpose(3, 2, 0, 1, 4)).reshape(NF, 128, 2 * KD * 128)


def prep_w2(w_out):
    return np.ascontiguousarray(w_out).reshape(NF, 128, D)


def prep_colblocks(w, nb, bw):
    s = w.reshape(KD, 128, nb, bw)
    return np.ascontiguousarray(s.transpose(2, 1, 0, 3)).reshape(nb, 128, KD * bw)


def prep_vec16(v):
    return np.ascontiguousarray(v.reshape(KD, 128).T)


def bc128(v):
    return np.ascontiguousarray(np.broadcast_to(v.reshape(1, -1), (128, v.size)))


OFF_M = 106 * 1024


class KM0(KT):
    def l0_heads(self, hn_all, y_src, wfm_d, wtm_d, cv_d, sm_d, lbl_d, msk_d, nsb):
        P = self.P
        self.fence()
        M = OFF_M
        hnb = [self.view(OFF_HNT + i * 16384, [128, KD, 512], BF16) for i in range(2)]
        o = M
        wfm = self.view(o, [128, KD, 1024], BF16); o += 32768
        wtm = self.view(o, [128, KD, 260], BF16); o += 8320
        pre = self.view(o, [128, 4, 520], F32); o += 4 * 520 * 4
        acc = self.view(o, [128, 512], F32); o += 2048
        xbc = self.view(o, [128, 4, 512], BF16); o += 4096
        tq = [self.view(o + i * 2048, [128, 512], F32) for i in range(5)]; o += 5 * 2048
        hq = self.view(o, [128, 2, 4, 512], BF16); o += 8192
        cdc = self.view(o, [128, 2, 8], F32); o += 64
        vtm = self.view(o, [64, 8, 256], BF16); o += 4096
        dtr = self.view(o, [64, 8, 4], F32); o += 128
        dta = self.view(o, [64, 8, 4], F32); o += 128
        cst = self.view(o, [128, 64], F32); o += 256
        msk = self.view(o, [64, 4, 64], F32); o += 1024
        mskb = self.view(o, [64, 64], BF16); o += 128
        prevT = self.view(o, [128, 256], F32); o += 1024
        prevTb = self.view(o, [128, 256], BF16); o += 512
        hprev = self.view(o, [128, 2, 128], F32); o += 1024
        hprevb = self.view(o, [128, 2, 128], BF16); o += 512
        def _ck(o0):
            d_ = {}
            for nm, shp, dt_ in (("R", [64, 4, 64], F32), ("dec", [64, 4, 64], F32), ("cbm", [64, 64], F32), ("wT", [64, 4, 64], BF16),
                                 ("xtm", [64, 256], F32), ("btm", [64, 128], BF16), ("xdt", [64, 4, 64], BF16), ("xde", [64, 4, 64], BF16),
                                 ("sm", [128, 16], F32), ("y", [64, 512], F32), ("yb", [64, 512], BF16)):
                n_ = 1
                for q_ in shp[1:]:
                    n_ *= q_
                d_[nm] = self.view(o0, shp, dt_)
                o0 += n_ * (2 if dt_ == BF16 else 4)
            return d_, o0
        c0_, o = _ck(o)
        ck = [c0_, c0_]
        attb = [self.view(o + i * 128, [64, 64], BF16) for i in range(2)]; o += 256
        ketm = [self.view(o + i * 256, [64, 128], BF16) for i in range(2)]; o += 512
        onesf = self.view(o, [128, 64], F32); o += 256
        ones64 = self.view(o, [64, 128], F32); o += 512
        assert o <= self.ARENA, o
        Us, Li, Ca, On = msk[:, 0, :], msk[:, 1, :], msk[:, 2, :], msk[:, 3, :]

        P.dma("pool", wfm.rearrange("p k c -> p (k c)"), wfm_d, writes=["wfm"])
        P.dma("pool", wtm.rearrange("p k c -> p (k c)"), wtm_d, writes=["wtm"])
        P.dma("sp", cst[:, 0:20], cv_d, writes=["cst"])
        P.dma("sp", cst[:, 20:32], sm_d, writes=["cst"])
        P.dma("sp", cst[:, 32:38], lbl_d, writes=["cst"])
        P.dma("sp", msk.rearrange("p a b -> p (a b)"), msk_d, writes=["msk"])
        P.op("dve", lambda e: e.tensor_copy(out=mskb, in_=Ca), reads=["msk"], writes=["mskb"])
        cvw = cst[:, 0:20]
        P.op("act", lambda e: e.activation(out=cst[:, 40:44], in_=cst[:, 24:28], func=AF.Exp), reads=["cst"], writes=["cst"])
        P.op("dve", lambda e: e.tensor_scalar(out=cst[:, 40:44], in0=cst[:, 40:44], scalar1=-1.0, scalar2=None, op0=ALU.mult),
             reads=["cst"], writes=["cst"])
        P.op("act", lambda e: e.activation(out=cst[:, 48:54], in_=cst[:, 32:38], func=AF.Exp), reads=["cst"], writes=["cst"])
        P.op("dve", lambda e: e.tensor_reduce(out=cst[:, 54:56], in_=cst[:, 48:54].rearrange("p (h l) -> p h l", h=2, l=3),
                                              axis=AX.X, op=ALU.add), reads=["cst"], writes=["cst"])
        P.op("dve", lambda e: e.reciprocal(out=cst[:, 54:56], in_=cst[:, 54:56]), reads=["cst"], writes=["cst"])
        P.op("dve", lambda e: e.tensor_tensor(out=cst[:, 44:46], in0=cst[:, 48:54].rearrange("p (h l) -> p h l", h=2, l=3)[:, :, 0],
                                              in1=cst[:, 54:56], op=ALU.mult), reads=["cst"], writes=["cst"])
        P.op("dve", lambda e: e.tensor_scalar(out=cst[:, 46:48], in0=cst[:, 44:46], scalar1=-1.0, scalar2=1.0,
                                              op0=ALU.mult, op1=ALU.add), reads=["cst"], writes=["cst"])
        P.op("pool", lambda e: e.memset(onesf, 1.0), writes=["onesf"])
        P.op("pool", lambda e: e.memset(ones64, 1.0), writes=["ones64"])
        for ap_, key in ((pre, "pre"), (prevT, "prevT"), (prevTb, "prevTb"), (hprev, "hprev"), (hprevb, "hprevb")):
            P.op("pool", lambda e, ap_=ap_: e.memset(ap_, 0.0), writes=[key])

        def load_hn(sb):
            r, hf = sb // 2, sb % 2
            src = hn_all[r * 2048:(r + 1) * 2048, hf * 512:(hf + 1) * 512].rearrange("(k p) t -> p k t", p=128)
            P.dma("sp", hnb[sb % 2], src, writes=["hnb%d" % (sb % 2)])

        load_hn(0)
        fmc = 0
        ykeys = []
        cut0 = float(self.cfg.get("cut0", 99))
        if cut0 <= 1:
            return ykeys
        for sb in range(nsb):
            hb = hnb[sb % 2]
            hk = "hnb%d" % (sb % 2)
            if sb + 1 < nsb:
                load_hn(sb + 1)
            def fm_mm(ci, bank, hb=hb):
                def mm(e, hb=hb):
                    ins = None
                    for k in range(KD):
                        ins = e.matmul(self.psv(bank), wfm[:, k, ci * 128:(ci + 1) * 128], hb[:, k, :], start=(k == 0), stop=(k == KD - 1))
                    return ins
                return mm
            for ci in range(4):
                bank = fmc % 2; fmc += 1
                P.op("pe", fm_mm(ci, bank), reads=["wfm", hk], writes=["bank%d" % bank])
                P.op("act", lambda e, ci=ci, bank=bank: e.activation(out=pre[:, ci, 8:520], in_=self.psv(bank), func=AF.Copy),
                     reads=["bank%d" % bank], writes=["pre"])
            for ci in range(4):
                c0 = ci * 5
                P.op("dve", lambda e, ci=ci, c0=c0: e.tensor_scalar(out=acc, in0=pre[:, ci, 5:517], scalar1=cvw[:, c0:c0 + 1],
                                                                    scalar2=cvw[:, c0 + 4:c0 + 5], op0=ALU.mult, op1=ALU.add),
                     reads=["pre", "cst"], writes=["acc"])
                for kk in range(1, 4):
                    P.op("dve", lambda e, ci=ci, c0=c0, kk=kk: e.scalar_tensor_tensor(
                        out=acc, in0=pre[:, ci, 5 + kk:5 + kk + 512], scalar=cvw[:, c0 + kk:c0 + kk + 1], in1=acc, op0=ALU.mult, op1=ALU.add),
                        reads=["pre", "cst", "acc"], writes=["acc"])
                P.op("act", lambda e, ci=ci: e.activation(out=xbc[:, ci, :], in_=acc, func=AF.Silu), reads=["acc"], writes=["xbc"])
            if cut0 <= 2:
                return ykeys
            P.op("pool", lambda e: e.tensor_copy(out=pre[:, :, 5:8], in_=pre[:, :, 517:520]), reads=["pre", "acc"], writes=["pre"])
            for hd in range(2):
                bq = fmc % 2; fmc += 1
                P.op("pe", fm_mm(4 + hd, bq), reads=["wfm", hk], writes=["bank%d" % bq])
                P.op("act", lambda e, bq=bq: e.activation(out=tq[0], in_=self.psv(bq), func=AF.Silu), reads=["bank%d" % bq], writes=["tq0"])
                bf = fmc % 2; fmc += 1
                P.op("pe", fm_mm(6 + hd, bf), reads=["wfm", hk], writes=["bank%d" % bf])
                P.op("act", lambda e, bf=bf: e.activation(out=tq[1], in_=self.psv(bf), func=AF.Sigmoid), reads=["bank%d" % bf], writes=["tq1"])
                P.op("dve", lambda e, hd=hd: e.tensor_scalar(out=tq[1], in0=tq[1], scalar1=cst[:, 46 + hd:47 + hd], scalar2=cst[:, 44 + hd:45 + hd],
                                                           op0=ALU.mult, op1=ALU.add), reads=["tq1", "cst"], writes=["tq1"])
                P.op("act", lambda e: e.activation(out=tq[2], in_=tq[1], func=AF.Ln), reads=["tq1"], writes=["tq2"])
                P.op("dve", lambda e: e.tensor_scalar(out=tq[1], in0=tq[1], scalar1=-1.0, scalar2=1.0, op0=ALU.mult, op1=ALU.add),
                     reads=["tq1"], writes=["tq1"])
                for cc in range(8):
                    sl = slice(cc * 64, (cc + 1) * 64)
                    P.op("dve", lambda e, sl=sl: e.tensor_tensor_scan(out=tq[3][:, sl], data0=onesf,
                                                                      data1=tq[2][:, sl], initial=0.0, op0=ALU.mult, op1=ALU.add),
                         reads=["tq2", "onesf"], writes=["tq3"])
                c3 = tq[3].rearrange("p (c l) -> p c l", c=8, l=64)
                mid = c3[:, :, 31:32].broadcast_to([128, 8, 64])
                last = c3[:, :, 63:64].broadcast_to([128, 8, 64])
                v3 = lambda a: a.rearrange("p (c l) -> p c l", c=8, l=64)
                P.op("act", lambda e, hd=hd: e.activation(out=cdc[:, hd, :], in_=c3[:, :, 63], func=AF.Exp), reads=["tq3"], writes=["cdc"])
                P.op("act", lambda e: e.activation(out=tq[2], in_=tq[3], func=AF.Exp), reads=["tq3"], writes=["tq2"])
                P.op("dve", lambda e, hd=hd: e.tensor_tensor(out=hq[:, hd, 3, :], in0=tq[0], in1=tq[2], op=ALU.mult), reads=["tq0", "tq2"], writes=["hq"])
                P.op("dve", lambda e: e.tensor_tensor(out=v3(tq[2]), in0=c3, in1=mid, op=ALU.subtract), reads=["tq3"], writes=["tq2"])
                P.op("act", lambda e: e.activation(out=tq[4], in_=tq[2], func=AF.Exp), reads=["tq2"], writes=["tq4"])
                P.op("dve", lambda e, hd=hd: e.tensor_tensor(out=hq[:, hd, 0, :], in0=tq[0], in1=tq[4], op=ALU.mult), reads=["tq0", "tq4"], writes=["hq"])
                P.op("act", lambda e: e.activation(out=tq[4], in_=tq[2], func=AF.Exp, scale=-1.0), reads=["tq2"], writes=["tq4"])
                P.op("dve", lambda e, hd=hd: e.tensor_tensor(out=hq[:, hd, 1, :], in0=tq[1], in1=tq[4], op=ALU.mult), reads=["tq1", "tq4"], writes=["hq"])
                P.op("dve", lambda e: e.tensor_tensor(out=v3(tq[2]), in0=last, in1=c3, op=ALU.subtract), reads=["tq3"], writes=["tq2"])
                P.op("act", lambda e: e.activation(out=tq[4], in_=tq[2], func=AF.Exp), reads=["tq2"], writes=["tq4"])
                P.op("dve", lambda e, hd=hd: e.tensor_tensor(out=hq[:, hd, 2, :], in0=tq[1], in1=tq[4], op=ALU.mult), reads=["tq1", "tq4"], writes=["hq"])
            if cut0 <= 3:
                return ykeys
            for cc in range(8):
                bank = 2 + cc % 2

                def mm(e, cc=cc, bank=bank, hb=hb):
                    ins = None
                    for k in range(KD):
                        ins = e.matmul(self.psv(bank)[0:64, 0:260], hb[:, k, cc * 64:(cc + 1) * 64], wtm[:, k, :], start=(k == 0), stop=(k == KD - 1))
                    return ins
                P.op("pe", mm, reads=["wtm", hk], writes=["bank%d" % bank])
                P.op("act", lambda e, cc=cc, bank=bank: e.activation(out=vtm[:, cc, :], in_=self.psv(bank)[0:64, 0:256], func=AF.Copy),
                     reads=["bank%d" % bank], writes=["vtm"])
                P.op("dve", lambda e, cc=cc, bank=bank: e.tensor_tensor(out=dtr[:, cc, :], in0=self.psv(bank)[0:64, 256:260], in1=cst[0:64, 20:24], op=ALU.add),
                     reads=["bank%d" % bank, "cst"], writes=["dtr"])
            P.op("act", lambda e: e.activation(out=dtr, in_=dtr, func=AF.Exp), reads=["dtr"], writes=["dtr"])
            P.op("dve", lambda e: e.tensor_scalar(out=dtr, in0=dtr, scalar1=1.0, scalar2=None, op0=ALU.add), reads=["dtr"], writes=["dtr"])
            P.op("act", lambda e: e.activation(out=dtr, in_=dtr, func=AF.Ln), reads=["dtr"], writes=["dtr"])
            P.op("act", lambda e: e.activation(out=cst[:, 60:64], in_=cst[:, 56:60], func=AF.Exp), reads=["cstd"], writes=["cstd"])
            P.op("dve", lambda e: e.tensor_tensor(out=dta, in0=dtr, in1=cst[0:64, 40:44].unsqueeze(1).broadcast_to([64, 8, 4]), op=ALU.mult),
                 reads=["dtr", "cst"], writes=["dta"])
            if cut0 <= 4:
                return ykeys
            for cc in range(8):
                C = ck[cc % 2]
                kz = "c0"
                sl = slice(cc * 64, (cc + 1) * 64)
                gtok = sb * 512 + cc * 64
                pst = self.psv(4, BF16)

                def trs(e, sl=sl, pst=pst):
                    ins = None
                    for i in range(3):
                        ins = e.transpose(pst[0:64, i * 128:(i + 1) * 128], xbc[:, i, sl], self.ident)
                    return ins
                P.op("pe", trs, reads=["xbc", "ident"], writes=["bank4"])
                P.op("dve", lambda e, C=C, pst=pst: e.tensor_copy(out=C["xtm"], in_=pst[0:64, 0:256]), reads=["bank4"], writes=[kz + "xtm"])
                P.op("dve", lambda e, C=C, pst=pst: e.tensor_copy(out=C["btm"], in_=pst[0:64, 256:384]), reads=["bank4"], writes=[kz + "btm"])
                if cut0 <= 4.1:
                    return ykeys
                P.op("dve", lambda e, C=C, cc=cc: e.tensor_tensor(out=C["R"], in0=dta[:, cc, :].unsqueeze(2).broadcast_to([64, 4, 64]),
                                                                in1=Li.unsqueeze(1).broadcast_to([64, 4, 64]), op=ALU.mult),
                     reads=["dta", "msk"], writes=[kz + "R"])
                ps5 = self.psv(5)

                def mm5(e, C=C, cc=cc, sl=sl, ps5=ps5):
                    e.matmul(ps5[0:64, 0:256], Us, C["R"].rearrange("p h l -> p (h l)"), start=True, stop=True)
                    e.matmul(ps5[0:64, 256:260], Li, dta[:, cc, :], start=True, stop=True)
                    e.matmul(ps5[:, 260:264], ones64, dta[:, cc, :], start=True, stop=True)
                    return e.matmul(ps5[0:64, 320:384], xbc[:, 2, sl], xbc[:, 3, sl], start=True, stop=True)
                P.op("pe", mm5, reads=[kz + "R", "msk", "dta", "xbc", "ones64"], writes=["bank5"])
                if cut0 <= 4.2:
                    return ykeys
                sm = C["sm"]
                P.op("dve", lambda e, sm=sm, ps5=ps5: e.tensor_copy(out=sm[:, 4:8], in_=ps5[:, 260:264]), reads=["bank5"], writes=[kz + "sm"])
                if cut0 <= 4.21:
                    return ykeys
                P.op("dve", lambda e, sm=sm, ps5=ps5: e.tensor_copy(out=sm[0:64, 0:4], in_=ps5[0:64, 256:260]), reads=["bank5"], writes=[kz + "sm"])
                if cut0 <= 4.22:
                    return ykeys
                P.op("act", lambda e, C=C, ps5=ps5: e.activation(out=C["dec"].rearrange("p h l -> p (h l)"), in_=ps5[0:64, 0:256], func=AF.Exp),
                     reads=["bank5"], writes=[kz + "dec"])
                if cut0 <= 4.23:
                    return ykeys
                P.op("dve", lambda e, C=C, ps5=ps5: e.tensor_tensor(out=C["cbm"], in0=ps5[0:64, 320:384], in1=Ca, op=ALU.mult),
                     reads=["bank5", "msk"], writes=[kz + "cbm"])
                if cut0 <= 4.24:
                    return ykeys
                P.op("dve", lambda e, C=C: e.tensor_tensor(out=C["wT"], in0=C["dec"], in1=C["cbm"].unsqueeze(1).broadcast_to([64, 4, 64]), op=ALU.mult),
                     reads=[kz + "dec", kz + "cbm"], writes=[kz + "wT"])
                if cut0 <= 4.25:
                    return ykeys
                P.op("act", lambda e, sm=sm: e.activation(out=sm[0:64, 8:12], in_=sm[0:64, 0:4], func=AF.Exp), reads=[kz + "sm"], writes=[kz + "sm"])
                if cut0 <= 4.26:
                    return ykeys
                P.op("dve", lambda e, sm=sm: e.tensor_tensor(out=sm[0:64, 12:16], in0=sm[0:64, 4:8], in1=sm[0:64, 0:4], op=ALU.subtract),
                     reads=[kz + "sm"], writes=[kz + "sm"])
                if cut0 <= 4.27:
                    return ykeys
                P.op("act", lambda e, sm=sm: e.activation(out=sm[0:64, 12:16], in_=sm[0:64, 12:16], func=AF.Exp), reads=[kz + "sm"], writes=[kz + "sm"])
                if cut0 <= 4.28:
                    return ykeys
                P.op("act", lambda e, sm=sm: e.activation(out=sm[:, 4:8], in_=sm[:, 4:8], func=AF.Exp), reads=[kz + "sm"], writes=[kz + "sm"])
                if cut0 <= 4.29:
                    return ykeys
                if cut0 <= 4.3:
                    return ykeys
                x3 = C["xtm"].rearrange("p (h q) -> p h q", h=4, q=64)
                P.op("dve", lambda e, C=C, cc=cc, x3=x3: e.tensor_tensor(out=C["xdt"], in0=x3, in1=dtr[:, cc, :].unsqueeze(2).broadcast_to([64, 4, 64]), op=ALU.mult),
                     reads=[kz + "xtm", "dtr"], writes=[kz + "xdt"])
                P.op("dve", lambda e, C=C, sm=sm: e.tensor_tensor(out=C["xde"], in0=C["xdt"], in1=sm[0:64, 12:16].unsqueeze(2).broadcast_to([64, 4, 64]), op=ALU.mult),
                     reads=[kz + "xdt", kz + "sm"], writes=[kz + "xde"])
                if cut0 <= 4.5:
                    return ykeys
                ps6 = self.psv(6)

                def mm6(e, C=C, sl=sl, ps6=ps6):
                    for h in range(4):
                        e.matmul(ps6[0:64, h * 64:(h + 1) * 64], C["wT"][:, h, :], C["xdt"][:, h, :], start=True, stop=True)
                    return e.matmul(ps6[0:64, 256:512], xbc[:, 3, sl], prevTb, start=True, stop=True)
                P.op("pe", mm6, reads=[kz + "wT", kz + "xdt", "xbc", "prevTb"], writes=["bank6"])
                y3 = C["y"][:, 0:256].rearrange("p (h q) -> p h q", h=4, q=64)
                P.op("dve", lambda e, y3=y3, sm=sm, ps6=ps6: e.tensor_tensor(out=y3, in0=ps6[0:64, 256:512].rearrange("p (h q) -> p h q", h=4, q=64),
                                                                            in1=sm[0:64, 8:12].unsqueeze(2).broadcast_to([64, 4, 64]), op=ALU.mult),
                     reads=["bank6", kz + "sm"], writes=[kz + "y"])
                P.op("dve", lambda e, C=C, ps6=ps6: e.tensor_tensor(out=C["y"][:, 0:256], in0=C["y"][:, 0:256], in1=ps6[0:64, 0:256], op=ALU.add),
                     reads=["bank6", kz + "y"], writes=[kz + "y"])
                P.op("pool", lambda e, C=C, x3=x3: e.tensor_tensor(out=C["dec"], in0=x3, in1=cst[0:64, 28:32].unsqueeze(2).broadcast_to([64, 4, 64]), op=ALU.mult),
                     reads=[kz + "xtm", "cst", kz + "wT"], writes=[kz + "dec"])
                P.op("pool", lambda e, C=C: e.tensor_tensor(out=C["y"][:, 0:256], in0=C["y"][:, 0:256], in1=C["dec"].rearrange("p h l -> p (h l)"), op=ALU.add),
                     reads=[kz + "y", kz + "dec"], writes=[kz + "y"])
                if cut0 <= 4.6:
                    return ykeys
                ps7 = self.psv(7)
                P.op("pe", lambda e, C=C, ps7=ps7: e.matmul(ps7[:, 0:256], C["btm"], C["xde"].rearrange("p h q -> p (h q)"), start=True, stop=True),
                     reads=[kz + "btm", kz + "xde"], writes=["bank7"])
                p3 = prevT.rearrange("p (h q) -> p h q", h=4, q=64)
                P.op("dve", lambda e, sm=sm, p3=p3: e.tensor_tensor(out=p3, in0=p3, in1=sm[:, 4:8].unsqueeze(2).broadcast_to([128, 4, 64]), op=ALU.mult),
                     reads=["prevT", kz + "sm", "bank6"], writes=["prevT"])
                P.op("dve", lambda e, ps7=ps7: e.tensor_tensor(out=prevT, in0=prevT, in1=ps7[:, 0:256], op=ALU.add), reads=["prevT", "bank7"], writes=["prevT"])
                P.op("act", lambda e: e.activation(out=prevTb, in_=prevT, func=AF.Copy), reads=["prevT", "bank6"], writes=["prevTb"])
                if cut0 <= 5:
                    return ykeys
                for hd in range(2):
                    ab = attb[hd]
                    kb = ketm[hd]
                    hz = "hg%d" % hd
                    ps4 = self.psv(4)
                    P.op("pe", lambda e, hd=hd, sl=sl, ps4=ps4: e.matmul(ps4[0:64, 256:320], hq[:, hd, 1, sl], hq[:, hd, 0, sl], start=True, stop=True),
                         reads=["hq"], writes=["bank4"])
                    P.op("dve", lambda e, ab=ab, ps4=ps4: e.tensor_tensor(out=ab, in0=ps4[0:64, 256:320], in1=Ca, op=ALU.mult),
                         reads=["bank4", "msk"], writes=[hz + "att"])
                    ps7b = self.psv(7, BF16)
                    P.op("pe", lambda e, hd=hd, sl=sl, ps7b=ps7b: e.transpose(ps7b[0:64, 512:640], hq[:, hd, 2, sl], self.ident),
                         reads=["hq", "ident"], writes=["bank7"])
                    P.op("act", lambda e, kb=kb, ps7b=ps7b: e.activation(out=kb, in_=ps7b[0:64, 512:640], func=AF.Copy), reads=["bank7"], writes=[hz + "ke"])
                    vsl = vtm[:, cc, hd * 128:(hd + 1) * 128]

                    def mmo(e, ab=ab, hd=hd, sl=sl, vsl=vsl, ps6=ps6):
                        e.matmul(ps6[0:64, 0:128], ab, vsl, start=True, stop=False)
                        return e.matmul(ps6[0:64, 0:128], hq[:, hd, 3, sl], hprevb[:, hd, :], start=False, stop=True)
                    P.op("pe", mmo, reads=[hz + "att", "vtm", "hq", "hprevb"], writes=["bank6"])
                    P.op("act", lambda e, C=C, hd=hd, ps6=ps6: e.activation(out=C["y"][:, 256 + hd * 128:256 + (hd + 1) * 128], in_=ps6[0:64, 0:128], func=AF.Copy),
                         reads=["bank6"], writes=[kz + "y"])
                    P.op("pe", lambda e, kb=kb, vsl=vsl, ps5=ps5: e.matmul(ps5[:, 384:512], kb, vsl, start=True, stop=True),
                         reads=[hz + "ke", "vtm"], writes=["bank5"])
                    P.op("dve", lambda e, hd=hd, cc=cc, ps5=ps5: e.scalar_tensor_tensor(out=hprev[:, hd, :], in0=hprev[:, hd, :], scalar=cdc[:, hd, cc:cc + 1],
                                                                                       in1=ps5[:, 384:512], op0=ALU.mult, op1=ALU.add),
                         reads=["hprev", "cdc", "bank5", "bank6"], writes=["hprev"])
                    P.op("act", lambda e, hd=hd: e.activation(out=hprevb[:, hd, :], in_=hprev[:, hd, :], func=AF.Copy), reads=["hprev", "bank6"], writes=["hprevb"])
                if cut0 <= 6:
                    return ykeys
                P.op("dve", lambda e, C=C: e.tensor_copy(out=C["yb"], in_=C["y"]), reads=[kz + "y"], writes=[kz + "yb"])
                P.dma("sp", y_src[gtok:gtok + 64, :], C["yb"], reads=[kz + "yb"], writes=["y_src%d" % (gtok // 64)])
                ykeys.append("y_src%d" % (gtok // 64))
        return ykeys


def l0_masks():
    j = np.arange(64)
    Us = (j[:, None] > j[None, :]).astype(np.float32)
    Li = (j[:, None] <= j[None, :]).astype(np.float32)
    return np.ascontiguousarray(np.concatenate([Us, Li, Li, np.ones((64, 64), np.float32)], axis=1))


def l0_head_inputs(inp, c):
    w = inp["ab_w_in"][0]
    gi = c // 2
    xs = slice(2048 + 256 * c, 2048 + 256 * c + 256)
    Bs = slice(4096 + 128 * gi, 4096 + 128 * gi + 128)
    Cs = slice(4608 + 128 * gi, 4608 + 128 * gi + 128)
    dts = slice(5120 + 4 * c, 5120 + 4 * c + 4)
    qs = slice(5152 + 256 * c, 5152 + 256 * c + 256)
    fs = slice(7200 + 256 * c, 7200 + 256 * c + 256)
    vs = slice(9248 + 256 * c, 9248 + 256 * c + 256)
    wfm = np.concatenate([w[:, xs], w[:, Bs], w[:, Cs], w[:, qs], w[:, fs]], axis=1)
    wtm = np.concatenate([w[:, vs], w[:, dts]], axis=1)
    wfm_t = np.ascontiguousarray(wfm.reshape(KD, 128, 1024).transpose(1, 0, 2)).reshape(128, KD * 1024)
    wtm_t = np.ascontiguousarray(wtm.reshape(KD, 128, 260).transpose(1, 0, 2)).reshape(128, KD * 260)
    cwt = inp["ssd_conv_w"][0]
    cbs = inp["ssd_conv_b"][0]
    chans = [np.arange(256 * c, 256 * c + 128), np.arange(256 * c + 128, 256 * c + 256),
             np.arange(2048 + 128 * gi, 2048 + 128 * gi + 128), np.arange(2560 + 128 * gi, 2560 + 128 * gi + 128)]
    cv = np.zeros((128, 20), np.float32)
    for i, ch in enumerate(chans):
        cv[:, i * 5:i * 5 + 4] = cwt[:, ch].T
        cv[:, i * 5 + 4] = cbs[ch]
    hs = slice(4 * c, 4 * c + 4)
    sm = bc128(np.concatenate([inp["ssd_dt_bias"][0][hs], inp["ssd_a_log"][0][hs], inp["ssd_d"][0][hs]]))
    lg = inp["hgrn_lb_logits"]
    lbl = np.zeros((128, 6), np.float32)
    for h in range(2):
        ch = np.arange(256 * c + 128 * h, 256 * c + 128 * h + 128)
        lbl[:, h * 3:(h + 1) * 3] = lg[:, ch].T
    return dict(wfm=wfm_t, wtm=wtm_t, cv=cv, sm=sm, lbl=lbl, msk=l0_masks())


class KM1(KM0):
    def gather(self, src, dst, rkeys, wkey):
        P = self.P
        if "cc" not in P.sem:
            P.sem["cc"] = self.es.enter_context(self.nc.semaphore("s_cc"))
            P.cnt["cc"] = 0
        P._wait("pool", P._deps(rkeys, [wkey]))
        P.cnt["cc"] += 1
        tok = ("cc", P.cnt["cc"])
        sem = P.sem["cc"]
        P.ops["pool"].append(lambda e: e.collective_compute("AllGather", ALU.bypass, replica_groups=[list(range(NCORES))],
                                                            ins=[src.opt()], outs=[dst.opt()]).then_inc(sem, 1))
        P.log.append(("pool", "cc", tok, tuple(rkeys), (wkey,)))
        P._record(tok, rkeys, [wkey])

    def hn_to_dram(self, hn_src):
        self.P.dma("sp", hn_src.rearrange("(k p) t -> p k t", p=128), self.hnT, reads=["hnT"], writes=["hn_src"])

    def tok_phase(self, layer, y_all, ycols, hn_src, sel_d, wzg, wn_d, wo, nko):
        P = self.P
        self.fence()
        CH = nko * 128
        M = OFF_M
        hnh = self.view(OFF_HNT, [128, KD, 512], BF16)
        yt = self.view(OFF_HNT + 16384, [128, 4, 512], BF16)
        sel = self.view(OFF_HNT + 20480, [128, 8, 128], BF16)
        ub = self.view(OFF_HNT + 22528, [128, 4096], BF16)
        junk = self.view(OFF_HNT + 30720, [128, 512], BF16)
        sz = self.view(M, [128, 4, 4096], BF16)
        ysel = self.view(M + 32768, [128, 4096], F32)
        wz = self.view(M + 49152, [128, KD, 512], BF16)
        uT = self.view(M + 65536, [128, 32, 512], BF16)
        wob = [self.view(M + i * 32768, [128, 32, 512], BF16) for i in range(2)]
        wn = self.stat[:, 32:64]
        st = self.stat2
        P.dma("pool", sel.rearrange("p a b -> p (a b)"), sel_d, writes=["sel"])
        if layer == 0:
            P.dma("sp", wn, wn_d, writes=["wn"])
        wo3 = wo.rearrange("(k p) d -> p k d", p=128)
        cnt = [0]
        for hf in range(2):
            if layer == 0:
                P.dma("sp", hnh, hn_src[:, hf * 512:(hf + 1) * 512].rearrange("(k p) t -> p k t", p=128), reads=["hn_src"], writes=["hnh"])
                for dblk in range(8):
                    P.dma("pool", wz.rearrange("p k c -> p (k c)"), wzg[dblk], writes=["wz"])
                    for tt in range(4):
                        b = cnt[0] % 2; cnt[0] += 1

                        def mm(e, b=b, tt=tt):
                            ins = None
                            for k in range(KD):
                                ins = e.matmul(self.psv(b), hnh[:, k, tt * 128:(tt + 1) * 128], wz[:, k, :], start=(k == 0), stop=(k == KD - 1))
                            return ins
                        P.op("pe", mm, reads=["hnh", "wz"], writes=["bank%d" % b])
                        P.op("act", lambda e, b=b, tt=tt, dblk=dblk: e.activation(out=sz[:, tt, dblk * 512:(dblk + 1) * 512], in_=self.psv(b), func=AF.Silu),
                             reads=["bank%d" % b], writes=["sz"])
            for tt in range(4):
                t = hf * 4 + tt
                for r in range(8):
                    b = 2 + r % 2

                    for ch2 in range(2):
                        rows = y_all[r * SEQ:(r + 1) * SEQ, :].rearrange("(c t p) d -> t p c d", c=8, t=8, p=128)[t][:, ch2 * 4:(ch2 + 1) * 4, :]
                        P.dma("sp", yt[:, :, 0:ycols], rows, reads=["y_all"], writes=["yt"])

                        def mm(e, b=b, ch2=ch2):
                            ins = None
                            for c4 in range(4):
                                c = ch2 * 4 + c4
                                ins = e.matmul(self.psv(b)[:, 0:ycols], sel[:, c, :], yt[:, c4, 0:ycols], start=(c == 0), stop=(c == 7))
                            return ins
                        P.op("pe", mm, reads=["sel", "yt"], writes=["bank%d" % b])
                    if layer == 0:
                        P.op("act", lambda e, b=b, r=r: e.activation(out=ysel[:, 256 * r:256 * r + 256], in_=self.psv(b)[:, 0:256], func=AF.Copy),
                             reads=["bank%d" % b], writes=["ysel"])
                        P.op("dve", lambda e, b=b, r=r: e.tensor_copy(out=ysel[:, 2048 + 256 * r:2048 + 256 * r + 256], in_=self.psv(b)[:, 256:512]),
                             reads=["bank%d" % b], writes=["ysel"])
                    else:
                        P.op("act", lambda e, b=b, r=r: e.activation(out=ub[:, 256 * r:256 * r + 256], in_=self.psv(b)[:, 0:256], func=AF.Copy),
                             reads=["bank%d" % b], writes=["ub"])
                if layer == 0:
                    P.op("dve", lambda e, tt=tt: e.tensor_tensor(out=ysel[:, 0:2048], in0=ysel[:, 0:2048], in1=sz[:, tt, 0:2048], op=ALU.mult),
                         reads=["ysel", "sz"], writes=["ysel"])
                    for gidx in range(4):
                        P.op("act", lambda e, gidx=gidx: e.activation(out=junk, in_=ysel[:, gidx * 512:(gidx + 1) * 512], func=AF.Square,
                                                                     accum_out=st[:, gidx:gidx + 1]), reads=["ysel"], writes=["junk", "tss"])
                    for hh in range(16):
                        P.op("act", lambda e, hh=hh: e.activation(out=junk[:, 0:128], in_=ysel[:, 2048 + hh * 128:2048 + (hh + 1) * 128], func=AF.Square,
                                                                 accum_out=st[:, 4 + hh:5 + hh]), reads=["ysel"], writes=["junk", "tss"])
                    P.op("dve", lambda e: e.tensor_scalar(out=st[:, 0:4], in0=st[:, 0:4], scalar1=1.0 / 512, scalar2=EPS, op0=ALU.mult, op1=ALU.add),
                         reads=["tss"], writes=["tss"])
                    P.op("dve", lambda e: e.tensor_scalar(out=st[:, 4:20], in0=st[:, 4:20], scalar1=1.0 / 128, scalar2=EPS, op0=ALU.mult, op1=ALU.add),
                         reads=["tss"], writes=["tss"])
                    P.op("act", lambda e: e.activation(out=st[:, 0:20], in_=st[:, 0:20], func=AF.Sqrt), reads=["tss"], writes=["tss"])
                    P.op("dve", lambda e: e.reciprocal(out=st[:, 0:20], in_=st[:, 0:20]), reads=["tss"], writes=["tss"])
                    P.op("dve", lambda e: e.tensor_tensor(out=ub[:, 0:2048].rearrange("p (g q) -> p g q", g=4, q=512),
                                                          in0=ysel[:, 0:2048].rearrange("p (g q) -> p g q", g=4, q=512),
                                                          in1=st[:, 0:4].unsqueeze(2).broadcast_to([128, 4, 512]), op=ALU.mult),
                         reads=["ysel", "tss"], writes=["ub"])
                    P.op("dve", lambda e: e.tensor_tensor(out=ysel[:, 2048:4096].rearrange("p (g q) -> p g q", g=16, q=128),
                                                          in0=ysel[:, 2048:4096].rearrange("p (g q) -> p g q", g=16, q=128),
                                                          in1=st[:, 4:20].unsqueeze(2).broadcast_to([128, 16, 128]), op=ALU.mult),
                         reads=["ysel", "tss"], writes=["ysel"])
                    P.op("dve", lambda e, tt=tt: e.tensor_tensor(out=ub[:, 2048:4096], in0=ysel[:, 2048:4096], in1=sz[:, tt, 2048:4096], op=ALU.mult),
                         reads=["ysel", "sz"], writes=["ub"])
                for q4 in range(nko // 8):
                    bank = 4 + q4 % 4
                    psb = self.psv(bank, BF16)

                    def tr(e, q4=q4, psb=psb):
                        ins = None
                        for j in range(8):
                            kk = q4 * 8 + j
                            ins = e.transpose(psb[:, j * 128:(j + 1) * 128], ub[:, kk * 128:(kk + 1) * 128], self.ident)
                        return ins
                    P.op("pe", tr, reads=["ub", "ident"], writes=["bank%d" % bank])
                    if layer == 0:
                        P.op("dve", lambda e, q4=q4, psb=psb, tt=tt: e.tensor_tensor(
                            out=uT[:, q4 * 8:(q4 + 1) * 8, tt * 128:(tt + 1) * 128], in0=psb.rearrange("p (a b) -> p a b", a=8, b=128),
                            in1=wn[:, q4 * 8:(q4 + 1) * 8].unsqueeze(2).broadcast_to([128, 8, 128]), op=ALU.mult),
                            reads=["bank%d" % bank, "wn"], writes=["uT"])
                    else:
                        P.op("dve", lambda e, q4=q4, psb=psb, tt=tt: e.tensor_copy(
                            out=uT[:, q4 * 8:(q4 + 1) * 8, tt * 128:(tt + 1) * 128], in_=psb.rearrange("p (a b) -> p a b", a=8, b=128)),
                            reads=["bank%d" % bank], writes=["uT"])
            for db in range(4):
                wb = wob[db % 2]
                P.dma("pool", wb[:, 0:nko, :], wo3[:, :, db * 512:(db + 1) * 512], reads=["sz", "ysel", "wz"], writes=["wob%d" % (db % 2), "sz", "ysel", "wz"])
                for tt in range(4):
                    t = hf * 4 + tt
                    b = cnt[0] % 2; cnt[0] += 1

                    def mm(e, b=b, tt=tt, wb=wb):
                        ins = None
                        for k in range(nko):
                            ins = e.matmul(self.psv(b), uT[:, k, tt * 128:(tt + 1) * 128], wb[:, k, :], start=(k == 0), stop=(k == nko - 1))
                        return ins
                    P.op("pe", mm, reads=["uT", "wob%d" % (db % 2), "sz", "ysel", "wz"], writes=["bank%d" % b])
                    hs = self.h[:, t, db * 512:(db + 1) * 512]
                    P.op("dve", lambda e, b=b, hs=hs: e.tensor_tensor(out=hs, in0=hs, in1=self.psv(b), op=ALU.add),
                         reads=["bank%d" % b, "h%d" % t], writes=["h%d" % t])


class KM2(KM1):
    def l1_heads(self, hn_all, y_src, wfm_d, wtm_d, bf_d, tri_d, nsb):
        P = self.P
        self.fence()
        SC = 128.0 ** -0.5
        hnb = self.view(OFF_HNT, [128, KD, 512], BF16)
        kst = [self.view(OFF_HNT + 16384, [128, SEQ], BF16)]
        o = OFF_M
        kst.append(self.view(o, [128, SEQ], BF16)); o += 16384
        wfm = self.view(o, [128, KD, 512], BF16); o += 16384
        wtm = self.view(o, [128, KD, 258], BF16); o += 8256
        vst = self.view(o, [128, 64, 2, 144], BF16); o += 64 * 2 * 144 * 2
        qsb = self.view(o, [128, 2, 512], BF16); o += 2048
        dcT = self.view(o, [128, 64, 2], F32); o += 512
        tota = self.view(o, [128, 64, 2], F32); o += 512
        bias = [self.view(o + i * 256, [128, 64], F32) for i in range(2)]; o += 512
        tri = self.view(o, [128, 128], F32); o += 512
        trib = self.view(o, [128, 128], BF16); o += 256
        onesm = self.view(o, [128, 128], F32); o += 512
        pb = [self.view(o + i * 256, [128, 128], BF16) for i in range(4)]; o += 1024
        sm = self.view(o, [128, 16], F32); o += 64
        ob = [self.view(o + i * 512, [128, 256], BF16) for i in range(2)]; o += 1024
        assert o <= self.ARENA, o
        P.dma("pool", wfm.rearrange("p k c -> p (k c)"), wfm_d, writes=["wfm"])
        P.dma("pool", wtm.rearrange("p k c -> p (k c)"), wtm_d, writes=["wtm"])
        P.dma("sp", sm[:, 0:2], bf_d, writes=["sm"])
        P.dma("sp", tri, tri_d, writes=["tri"])
        P.op("dve", lambda e: e.tensor_copy(out=trib, in_=tri), reads=["tri"], writes=["trib"])
        P.op("pool", lambda e: e.memset(onesm, 1.0), writes=["onesm"])
        P.op("pool", lambda e: e.memset(vst, 1.0), writes=["vst"])
        scnt = [0]
        ocnt = [0]
        ykeys = []
        cut = float(self.cfg.get("cut", 99))
        cutb = float(self.cfg.get("cutb", 99))
        if cut <= 1:
            return ykeys
        for sb in range(nsb):
            r, hf = sb // 2, sb % 2
            src = hn_all[r * 2048:(r + 1) * 2048, hf * 512:(hf + 1) * 512].rearrange("(k p) t -> p k t", p=128)
            P.dma("sp", hnb, src, writes=["hnb"])
            for ci in range(4):
                bank = ci % 2

                def mm(e, ci=ci, bank=bank):
                    ins = None
                    for k in range(KD):
                        ins = e.matmul(self.psv(bank), wfm[:, k, ci * 128:(ci + 1) * 128], hnb[:, k, :], start=(k == 0), stop=(k == KD - 1))
                    return ins
                P.op("pe", mm, reads=["wfm", "hnb"], writes=["bank%d" % bank])
                if ci < 2:
                    P.op("act", lambda e, ci=ci, bank=bank: e.activation(out=qsb[:, ci, :], in_=self.psv(bank), func=AF.Copy),
                         reads=["bank%d" % bank], writes=["qsb"])
                else:
                    P.op("dve", lambda e, ci=ci, bank=bank, sb=sb: e.tensor_copy(out=kst[ci - 2][:, sb * 512:(sb + 1) * 512], in_=self.psv(bank)),
                         reads=["bank%d" % bank], writes=["kst"])
            if cut <= 2:
                return ykeys
            for tt in range(int(self.cfg.get("ttmax", 4))):
                tile_ = sb * 4 + tt

                def mm(e, tt=tt):
                    ins = None
                    for k in range(KD):
                        ins = e.matmul(self.psv(2)[:, 0:258], hnb[:, k, tt * 128:(tt + 1) * 128], wtm[:, k, :], start=(k == 0), stop=(k == KD - 1))
                    return ins
                P.op("pe", mm, reads=["wtm", "hnb"], writes=["bank2"])
                if cut <= 2.1 or (tt >= 1 and cutb <= 2.1):
                    return ykeys
                P.op("dve", lambda e, tile_=tile_: e.tensor_copy(out=vst[:, tile_, :, 0:128], in_=self.psv(2)[:, 0:256].rearrange("p (h d) -> p h d", h=2, d=128)),
                     reads=["bank2"], writes=["vst"])
                if cut <= 2.2 or (tt >= 1 and cutb <= 2.2):
                    return ykeys
                P.op("dve", lambda e: e.tensor_tensor(out=sm[:, 2:4], in0=self.psv(2)[:, 256:258], in1=sm[:, 0:2], op=ALU.add),
                     reads=["bank2", "sm"], writes=["sm"])
                P.op("act", lambda e: e.activation(out=sm[:, 2:4], in_=sm[:, 2:4], func=AF.Exp, scale=-1.0), reads=["sm"], writes=["sm"])
                P.op("dve", lambda e: e.tensor_scalar(out=sm[:, 2:4], in0=sm[:, 2:4], scalar1=1.0, scalar2=None, op0=ALU.add), reads=["sm"], writes=["sm"])
                P.op("act", lambda e: e.activation(out=sm[:, 2:4], in_=sm[:, 2:4], func=AF.Ln), reads=["sm"], writes=["sm"])
                P.op("dve", lambda e: e.tensor_scalar(out=sm[:, 4:6], in0=sm[:, 2:4], scalar1=-1.0, scalar2=None, op0=ALU.mult), reads=["sm"], writes=["sm"])

                if cut <= 2.4 or (tt >= 1 and cutb <= 2.4):
                    return ykeys

                def mm3(e):
                    e.matmul(self.psv(3)[:, 0:2], tri, sm[:, 4:6], start=True, stop=True)
                    return e.matmul(self.psv(3)[:, 2:4], onesm, sm[:, 4:6], start=True, stop=True)
                P.op("pe", mm3, reads=["tri", "onesm", "sm"], writes=["bank3"])
                if cut <= 2.6 or (tt >= 1 and cutb <= 2.6):
                    return ykeys
                if tile_ == 0:
                    P.op("dve", lambda e: e.tensor_copy(out=dcT[:, 0, :], in_=self.psv(3)[:, 0:2]), reads=["bank3"], writes=["dcT"])
                    P.op("dve", lambda e: e.tensor_copy(out=tota[:, 0, :], in_=self.psv(3)[:, 2:4]), reads=["bank3"], writes=["tota"])
                else:
                    P.op("dve", lambda e, tile_=tile_: e.tensor_tensor(out=dcT[:, tile_, :], in0=self.psv(3)[:, 0:2], in1=tota[:, tile_ - 1, :], op=ALU.add),
                         reads=["bank3", "tota"], writes=["dcT"])
                    P.op("dve", lambda e, tile_=tile_: e.tensor_tensor(out=tota[:, tile_, :], in0=self.psv(3)[:, 2:4], in1=tota[:, tile_ - 1, :], op=ALU.add),
                         reads=["bank3", "tota"], writes=["tota"])
                if cut <= 2.8 or (tt >= 1 and cutb <= 2.8):
                    return ykeys
            if cut <= 3:
                return ykeys
            for tt in range(4):
                qt = sb * 4 + tt
                obuf = ob[qt % 2]
                for hd in range(2):
                    bi_ = bias[hd]
                    P.op("dve", lambda e, qt=qt, hd=hd, bi_=bi_: e.tensor_scalar(out=bi_[:, 0:qt + 1], in0=dcT[:, 0:qt + 1, hd], scalar1=-1.0,
                                                                              scalar2=tota[:, qt, hd:hd + 1], op0=ALU.mult, op1=ALU.add),
                         reads=["dcT", "tota"], writes=["bias%d" % hd])
                    bo = 6 + ocnt[0] % 2; ocnt[0] += 1
                    for kb in range(qt + 1):
                        bs = 4 + scnt[0] % 2
                        pbuf = pb[scnt[0] % 4]
                        pk = "pb%d" % (scnt[0] % 4)
                        scnt[0] += 1
                        P.op("pe", lambda e, bs=bs, hd=hd, kb=kb, tt=tt: e.matmul(self.psv(bs)[:, 0:128], kst[hd][:, kb * 128:(kb + 1) * 128],
                                                                               qsb[:, hd, tt * 128:(tt + 1) * 128], start=True, stop=True),
                             reads=["kst", "qsb"], writes=["bank%d" % bs])
                        P.op("act", lambda e, bs=bs, pbuf=pbuf, bi_=bi_, kb=kb: e.activation(out=pbuf, in_=self.psv(bs)[:, 0:128], func=AF.Exp,
                                                                                         scale=SC, bias=bi_[:, kb:kb + 1]),
                             reads=["bank%d" % bs, "bias%d" % hd], writes=[pk])
                        if kb == qt:
                            P.op("dve", lambda e, pbuf=pbuf: e.tensor_tensor(out=pbuf, in0=pbuf, in1=trib, op=ALU.mult), reads=[pk, "trib"], writes=[pk])
                        P.op("pe", lambda e, bo=bo, pbuf=pbuf, kb=kb, hd=hd, qt=qt: e.matmul(self.psv(bo)[:, 0:129], pbuf, vst[:, kb, hd, 0:129],
                                                                                       start=(kb == 0), stop=(kb == qt)),
                             reads=[pk, "vst"], writes=["bank%d" % bo])
                    P.op("dve", lambda e, bo=bo, hd=hd: e.reciprocal(out=sm[:, 8 + hd:9 + hd], in_=self.psv(bo)[:, 128:129]), reads=["bank%d" % bo], writes=["rd%d" % hd])
                    P.op("act", lambda e, bo=bo, hd=hd, obuf=obuf: e.activation(out=obuf[:, hd * 128:(hd + 1) * 128], in_=self.psv(bo)[:, 0:128], func=AF.Copy,
                                                                              scale=sm[:, 8 + hd:9 + hd]),
                         reads=["bank%d" % bo, "rd%d" % hd], writes=["ob%d" % (qt % 2)])
                P.dma("sp", y_src[qt * 128:(qt + 1) * 128, :], obuf, reads=["ob%d" % (qt % 2)], writes=["y_src%d" % qt])
                ykeys.append("y_src%d" % qt)
        return ykeys


def l1_head_inputs(inp, c):
    w = inp["fox_w_in"][0]
    wfm = np.concatenate([w[:, 256 * c:256 * c + 256], w[:, 2048 + 256 * c:2048 + 256 * c + 256]], axis=1)
    wtm = np.concatenate([w[:, 4096 + 256 * c:4096 + 256 * c + 256], w[:, 6144 + 2 * c:6144 + 2 * c + 2]], axis=1)
    wfm_t = np.ascontiguousarray(wfm.reshape(KD, 128, 512).transpose(1, 0, 2)).reshape(128, KD * 512)
    wtm_t = np.ascontiguousarray(wtm.reshape(KD, 128, 258).transpose(1, 0, 2)).reshape(128, KD * 258)
    j = np.arange(128)
    tri = (j[:, None] <= j[None, :]).astype(np.float32)
    return dict(wfm1=wfm_t, wtm1=wtm_t, bf=bc128(inp["fox_b_f"][0][2 * c:2 * c + 2]), tri=np.ascontiguousarray(tri))


def build_program(nsb=16):
    nc = bass.Bass("TRN2", target_bir_lowering=False)
    es = ExitStack()
    k = KM2(nc, es, {})
    P = k.P
    k.setup_common()
    k.load_x()
    dn = k.din
    w1 = {(l, f): dn("w1_%d%d" % (l, f), [NF, 128, 4096]) for l in range(2) for f in range(2)}
    w2 = {(l, f): dn("w2_%d%d" % (l, f), [NF, 128, D]) for l in range(2) for f in range(2)}
    wg = [dn("wg_%d" % l, [4, 128, KD * 512]) for l in range(2)]
    wup = [dn("wup_%d" % l, [256, D]) for l in range(2)]
    pT = [dn("pT_%d" % l, [256, TOK]) for l in range(2)]
    wpost = [dn("wpost_%d" % l, [128, D]) for l in range(2)]
    wfin = dn("wfin", [128, D])
    sel = dn("sel", [128, 8 * 128])
    h0 = {n: dn(n, s) for n, s in (("wfm", [128, KD * 1024]), ("wtm", [128, KD * 260]), ("cv", [128, 20]), ("sm", [128, 12]),
                                   ("lbl", [128, 6]), ("msk", [64, 256]))}
    wzg = dn("wzg", [8, 128, KD * 512])
    wn = dn("wn", [128, 32])
    ab_wo = dn("ab_wo", [4096, D])
    h1 = {n: dn(n, s) for n, s in (("wfm1", [128, KD * 512]), ("wtm1", [128, KD * 258]), ("bf", [128, 2]), ("tri", [128, 128]))}
    fox_wo = dn("fox_wo", [D, D])
    out = k.dout("out", [TOK, D])
    mk = lambda n, s: nc.dram_tensor(n, s, BF16).ap()
    hn_src = mk("hn_src", [D, TOK])
    hn_all = mk("hn_all", [NCORES * D, TOK])
    y_src0 = mk("y_src0", [SEQ, 512])
    y_all0 = mk("y_all0", [NCORES * SEQ, 512])
    y_src1 = mk("y_src1", [SEQ, 256])
    y_all1 = mk("y_all1", [NCORES * SEQ, 256])
    k.ffn(w1[(0, 0)], w2[(0, 0)], 0)
    k.fence(); k.norm_T(1); k.hn_to_dram(hn_src); k.gather(hn_src, hn_all, ["hn_src"], "hn_all")
    yk = k.l0_heads(hn_all, y_src0, h0["wfm"], h0["wtm"], h0["cv"], h0["sm"], h0["lbl"], h0["msk"], nsb)
    k.gather(y_src0, y_all0, yk, "y_all")
    k.tok_phase(0, y_all0, 512, hn_src, sel, wzg, wn, ab_wo, 32)
    k.ffn(w1[(0, 1)], w2[(0, 1)], 2)
    k.ple(wg[0], wup[0], pT[0], wpost[0], 3)
    k.ffn(w1[(1, 0)], w2[(1, 0)], 4)
    k.fence(); k.norm_T(5); k.hn_to_dram(hn_src); k.gather(hn_src, hn_all, ["hn_src"], "hn_all")
    yk = k.l1_heads(hn_all, y_src1, h1["wfm1"], h1["wtm1"], h1["bf"], h1["tri"], nsb)
    k.gather(y_src1, y_all1, yk, "y_all")
    k.tok_phase(1, y_all1, 256, None, sel, None, None, fox_wo, 16)
    k.ffn(w1[(1, 1)], w2[(1, 1)], 6)
    k.ple(wg[1], wup[1], pT[1], wpost[1], 7)
    k.final(wfin, out)
    P.emit()
    return nc


def make_inputs(inp):
    f32 = lambda a: np.ascontiguousarray(np.asarray(a, dtype=np.float32))
    inp = {k_: f32(v) for k_, v in inp.items()}
    cw = np.zeros((128, 128), np.float32)
    for i, nm in enumerate(("ffn1_norm", "mix_norm", "ffn2_norm", "ple_gate_norm")):
        for l in range(2):
            cw[:, (l * 4 + i) * 16:(l * 4 + i + 1) * 16] = prep_vec16(inp[nm][l])
    shared = {"ident": np.eye(128, dtype=np.float32), "cw": cw, "wfin": bc128(inp["final_norm"])}
    for l in range(2):
        shared["w1_%d0" % l] = prep_w1(inp["ffn1_w_in"][l]); shared["w2_%d0" % l] = prep_w2(inp["ffn1_w_out"][l])
        shared["w1_%d1" % l] = prep_w1(inp["ffn2_w_in"][l]); shared["w2_%d1" % l] = prep_w2(inp["ffn2_w_out"][l])
        shared["wg_%d" % l] = prep_colblocks(inp["ple_w_gate"][l], 4, 512)
        shared["wup_%d" % l] = f32(inp["ple_w_up"][l])
        shared["wpost_%d" % l] = bc128(inp["ple_norm"][l])
    wab = inp["ab_w_in"][0]
    shared["wzg"] = np.concatenate([prep_colblocks(wab[:, 0:2048], 4, 512), prep_colblocks(wab[:, 11296:13344], 4, 512)], axis=0)
    shared["wn"] = np.ascontiguousarray(np.concatenate([prep_vec16(inp["ssd_norm"][0]), prep_vec16(inp["hgrn_norm"][0])], axis=1))
    shared["ab_wo"] = f32(inp["ab_w_out"][0])
    shared["fox_wo"] = f32(inp["fox_w_out"][0])
    maps = []
    for c in range(NCORES):
        m = dict(shared)
        m["x"] = f32(inp["x"][0, c * TOK:(c + 1) * TOK])
        for l in range(2):
            m["pT_%d" % l] = np.ascontiguousarray(inp["p"][l, 0, c * TOK:(c + 1) * TOK].T)
        s = np.zeros((128, 8, 128), np.float32)
        s[:, c, :] = np.eye(128, dtype=np.float32)
        m["sel"] = s.reshape(128, 1024)
        m.update(l0_head_inputs(inp, c))
        m.update(l1_head_inputs(inp, c))
        maps.append(m)
    return maps


_NC_CACHE = {}


def kernel(**inputs):
    if "nc" not in _NC_CACHE:
        _NC_CACHE["nc"] = build_program()
    nc = _NC_CACHE["nc"]
    maps = make_inputs(inputs)
    res = run_bass_kernel_spmd(nc, maps, core_ids=list(range(NCORES)))
    outs = [np.asarray(r["out"], dtype=np.float32) for r in res.results]
    return np.concatenate(outs, axis=0).reshape(1, SEQ, D)


def _new():
    nc = bass.Bass("TRN2", target_bir_lowering=False)
    es = ExitStack()
    k = KM2(nc, es, {})
    k.setup_common()
    return nc, k


def _ffn_in(k, l, f):
    return k.din("w1_%d%d" % (l, f), [NF, 128, 4096]), k.din("w2_%d%d" % (l, f), [NF, 128, D])


def _ple_in(k, l):
    return (k.din("wg_%d" % l, [4, 128, KD * 512]), k.din("wup_%d" % l, [256, D]), k.din("pT_%d" % l, [256, TOK]),
            k.din("wpost_%d" % l, [128, D]))


def build_A():
    nc, k = _new()
    k.load_x()
    w1, w2 = _ffn_in(k, 0, 0)
    k.ffn(w1, w2, 0)
    k.fence(); k.norm_T(1)
    hn_out = k.dout("hn_out", [D, TOK], BF16)
    k.hn_to_dram(hn_out)
    hout = k.dout("hout", [TOK, D])
    k.dump_h(hout)
    k.P.wait_all("sp", ["hn_src"])
    k.P.emit()
    return nc


def build_B():
    nc, k = _new()
    hn_all = k.din("hn_all", [NCORES * D, TOK], BF16)
    y_src = k.dout("y_src", [SEQ, 512], BF16)
    h0 = {n: k.din(n, s) for n, s in (("wfm", [128, KD * 1024]), ("wtm", [128, KD * 260]), ("cv", [128, 20]), ("sm", [128, 12]),
                                      ("lbl", [128, 6]), ("msk", [64, 256]))}
    yk = k.l0_heads(hn_all, y_src, h0["wfm"], h0["wtm"], h0["cv"], h0["sm"], h0["lbl"], h0["msk"], 16)
    k.P.wait_all("sp", yk)
    k.P.emit()
    return nc


def build_C():
    nc, k = _new()
    k.load_x()
    y_all = k.din("y_all", [NCORES * SEQ, 512], BF16)
    hn_src = k.din("hn_src", [D, TOK], BF16)
    sel = k.din("sel", [128, 1024]); wzg = k.din("wzg", [8, 128, KD * 512]); wn = k.din("wn", [128, 32]); ab_wo = k.din("ab_wo", [4096, D])
    k.tok_phase(0, y_all, 512, hn_src, sel, wzg, wn, ab_wo, 32)
    w1, w2 = _ffn_in(k, 0, 1)
    k.ffn(w1, w2, 2)
    k.ple(*_ple_in(k, 0), 3)
    w1, w2 = _ffn_in(k, 1, 0)
    k.ffn(w1, w2, 4)
    k.fence(); k.norm_T(5)
    hn_out = k.dout("hn_out", [D, TOK], BF16)
    k.hn_to_dram(hn_out)
    hout = k.dout("hout", [TOK, D])
    k.dump_h(hout)
    k.P.wait_all("sp", ["hn_src"])
    k.P.emit()
    return nc


def build_D():
    nc, k = _new()
    hn_all = k.din("hn_all", [NCORES * D, TOK], BF16)
    y_src = k.dout("y_src", [SEQ, 256], BF16)
    h1 = {n: k.din(n, s) for n, s in (("wfm1", [128, KD * 512]), ("wtm1", [128, KD * 258]), ("bf", [128, 2]), ("tri", [128, 128]))}
    yk = k.l1_heads(hn_all, y_src, h1["wfm1"], h1["wtm1"], h1["bf"], h1["tri"], 16)
    k.P.wait_all("sp", yk)
    k.P.emit()
    return nc


def build_E():
    nc, k = _new()
    k.load_x()
    y_all = k.din("y_all", [NCORES * SEQ, 256], BF16)
    sel = k.din("sel", [128, 1024]); fox_wo = k.din("fox_wo", [D, D])
    k.tok_phase(1, y_all, 256, None, sel, None, None, fox_wo, 16)
    w1, w2 = _ffn_in(k, 1, 1)
    k.ffn(w1, w2, 6)
    k.ple(*_ple_in(k, 1), 7)
    out = k.dout("out", [TOK, D])
    k.final(k.din("wfin", [128, D]), out)
    k.P.emit()
    return nc


def _launch(nc, maps, extra=None):
    names = set()
    for alloc in nc.allocations:
        try:
            if alloc.kind == "ExternalInput":
                names.add(alloc.memorylocations[0].name)
        except Exception:
            pass
    ins = []
    for c in range(NCORES):
        m = {n: v for n, v in maps[c].items() if n in names}
        if extra is not None:
            m.update({n: v for n, v in extra[c].items() if n in names})
        ins.append(m)
    return run_bass_kernel_spmd(nc, ins, core_ids=list(range(NCORES))).results


def kernel_multi(**inputs):
    maps = make_inputs(inputs)
    rA = _launch(build_A(), maps)
    hn_all = np.concatenate([r["hn_out"] for r in rA], axis=0)
    ex = [{"hn_all": hn_all} for _ in range(NCORES)]
    rB = _launch(build_B(), maps, ex)
    y_all = np.concatenate([r["y_src"] for r in rB], axis=0)
    ex = [{"y_all": y_all, "hn_src": rA[c]["hn_out"], "x": rA[c]["hout"]} for c in range(NCORES)]
    rC = _launch(build_C(), maps, ex)
    hn_all = np.concatenate([r["hn_out"] for r in rC], axis=0)
    ex = [{"hn_all": hn_all} for _ in range(NCORES)]
    rD = _launch(build_D(), maps, ex)
    y_all = np.concatenate([r["y_src"] for r in rD], axis=0)
    ex = [{"y_all": y_all, "x": rC[c]["hout"]} for c in range(NCORES)]
    rE = _launch(build_E(), maps, ex)
    outs = [np.asarray(r["out"], dtype=np.float32) for r in rE]
    return np.concatenate(outs, axis=0).reshape(1, SEQ, D)


kernel_fused = kernel
kernel = kernel_multi
```
